# Optimizing a Trainium2 kernel written in Bass

```python
import jax, jax.numpy as jnp
from jax import lax
import numpy as np

D_MODEL = 1024
BATCH = 8
SEQ = 2048
DEPTH = 1

MEM_LEN = 256
HD = 64
SB_HEADS = 8
FOX_HEADS = 8
MEM_HEADS = 4
MEM_HD = 128
D_SB = SB_HEADS * HD
D_FOX = FOX_HEADS * HD
D_MEM = MEM_HEADS * MEM_HD
N_BRANCH = 3
D_FF = 4 * D_MODEL
BLOCK_Q = 128
EPS = 1e-6
NEG_INF = -1e30
SPLITS = (D_SB, D_SB, D_SB, D_FOX, D_FOX, D_FOX, FOX_HEADS, D_MEM, N_BRANCH * D_MODEL)
D_IN = sum(SPLITS)

kernel_name = "hybrid_stickbreak_fox_memory_block"


def split_columns(t, sizes):
    pieces = []
    start = 0
    for n in sizes:
        pieces.append(t[..., start:start + n])
        start += n
    return pieces


def rmsnorm(x, g):
    xf = x.astype(jnp.float32)
    y = xf * lax.rsqrt(jnp.mean(xf * xf, axis=-1, keepdims=True) + EPS)
    return (y * g.astype(jnp.float32)).astype(x.dtype)


def to_heads(t, n, d):
    b, s, _ = t.shape
    return t.reshape(b, s, n, d).transpose(0, 2, 1, 3)


def from_heads(t):
    b, h, s, d = t.shape
    return t.transpose(0, 2, 1, 3).reshape(b, s, h * d)


def stick_breaking_attention(q, k, v):
    s_len = q.shape[2]
    scale = HD ** -0.5
    outs = []
    for i in range(s_len // BLOCK_Q):
        q0 = i * BLOCK_Q
        kend = q0 + BLOCK_Q
        z = jnp.einsum('bhtd,bhsd->bhts', q[:, :, q0:kend], k[:, :, :kend]).astype(jnp.float32) * scale
        t_idx = q0 + jnp.arange(BLOCK_Q)[:, None]
        s_idx = jnp.arange(kend)[None, :]
        strict = s_idx < t_idx
        log_rem = jnp.where(strict, jax.nn.log_sigmoid(-z), 0.0)
        after = lax.cumsum(log_rem, axis=3, reverse=True) - log_rem
        w = jnp.where(strict, jnp.exp(jax.nn.log_sigmoid(z) + after), 0.0)
        outs.append(jnp.einsum('bhts,bhsd->bhtd', w.astype(v.dtype), v[:, :, :kend]))
    return jnp.concatenate(outs, axis=2)


def forgetting_attention(q, k, v, log_f):
    s_len = q.shape[2]
    scale = HD ** -0.5
    F = lax.cumsum(log_f.astype(jnp.float32), axis=2)
    outs = []
    for i in range(s_len // BLOCK_Q):
        q0 = i * BLOCK_Q
        kend = q0 + BLOCK_Q
        z = jnp.einsum('bhtd,bhsd->bhts', q[:, :, q0:kend], k[:, :, :kend]).astype(jnp.float32) * scale
        z = z + F[:, :, q0:kend, None] - F[:, :, None, :kend]
        causal = jnp.arange(kend)[None, :] <= (q0 + jnp.arange(BLOCK_Q)[:, None])
        p = jax.nn.softmax(jnp.where(causal, z, NEG_INF), axis=-1)
        outs.append(jnp.einsum('bhts,bhsd->bhtd', p.astype(v.dtype), v[:, :, :kend]))
    return jnp.concatenate(outs, axis=2)


def memory_attention(q, k, v):
    z = jnp.einsum('bhtd,bhmd->bhtm', q, k).astype(jnp.float32) * (MEM_HD ** -0.5)
    p = jax.nn.softmax(z, axis=-1)
    return jnp.einsum('bhtm,bhmd->bhtd', p.astype(v.dtype), v)


def setup_inputs(seed: int = 0) -> dict:
    key = jax.random.key(seed)
    ks = jax.random.split(key, 20)

    def w(k, shape, fan_in):
        return jax.random.normal(k, shape, jnp.float32) * fan_in ** -0.5

    def gain(k, shape):
        return 1.0 + 0.02 * jax.random.normal(k, shape, jnp.float32)

    L = DEPTH
    return {
        "x": jax.random.normal(ks[0], (BATCH, SEQ, D_MODEL), jnp.float32),
        "mem": jax.random.normal(ks[1], (BATCH, MEM_LEN, D_MODEL), jnp.float32),
        "g_mix_norm": gain(ks[2], (L, D_MODEL)),
        "g_mem_norm": gain(ks[3], (L, D_MODEL)),
        "w_in": w(ks[4], (L, D_MODEL, D_IN), D_MODEL),
        "b_forget": 3.0 + 0.5 * jax.random.normal(ks[5], (L, FOX_HEADS), jnp.float32),
        "g_fox_q": gain(ks[6], (L, HD)),
        "g_fox_k": gain(ks[7], (L, HD)),
        "g_mem_q": gain(ks[8], (L, MEM_HD)),
        "g_mem_k": gain(ks[9], (L, MEM_HD)),
        "w_mem_kv": w(ks[10], (L, D_MODEL, 2 * D_MEM), D_MODEL),
        "w_branch_sb": w(ks[11], (L, D_SB, D_MODEL), D_SB),
        "w_branch_fox": w(ks[12], (L, D_FOX, D_MODEL), D_FOX),
        "w_branch_mem": w(ks[13], (L, D_MEM, D_MODEL), D_MEM),
        "w_out": w(ks[14], (L, D_MODEL, D_MODEL), D_MODEL),
        "g_mlp_norm": gain(ks[15], (L, D_MODEL)),
        "w_ff_up": w(ks[16], (L, D_MODEL, D_FF), D_MODEL),
        "w_ff_down": w(ks[17], (L, D_FF, D_MODEL), D_FF),
    }


def reference(x, mem, g_mix_norm, g_mem_norm, w_in, b_forget, g_fox_q, g_fox_k, g_mem_q, g_mem_k,
              w_mem_kv, w_branch_sb, w_branch_fox, w_branch_mem, w_out, g_mlp_norm, w_ff_up, w_ff_down):
    b, s, _ = x.shape
    for l in range(DEPTH):
        h = rmsnorm(x, g_mix_norm[l])
        proj = jnp.einsum('bsd,de->bse', h, w_in[l])
        sb_q, sb_k, sb_v, fx_q, fx_k, fx_v, f_logit, m_q, gate_logit = split_columns(proj, SPLITS)

        o_sb = stick_breaking_attention(to_heads(sb_q, SB_HEADS, HD), to_heads(sb_k, SB_HEADS, HD),
                                        to_heads(sb_v, SB_HEADS, HD))

        fq = rmsnorm(to_heads(fx_q, FOX_HEADS, HD), g_fox_q[l])
        fk = rmsnorm(to_heads(fx_k, FOX_HEADS, HD), g_fox_k[l])
        log_f = jax.nn.log_sigmoid((f_logit + b_forget[l]).astype(jnp.float32)).transpose(0, 2, 1)
        o_fox = forgetting_attention(fq, fk, to_heads(fx_v, FOX_HEADS, HD), log_f)

        mh = rmsnorm(mem, g_mem_norm[l])
        mkv = jnp.einsum('bmd,de->bme', mh, w_mem_kv[l])
        mk, mv = split_columns(mkv, (D_MEM, D_MEM))
        mq = rmsnorm(to_heads(m_q, MEM_HEADS, MEM_HD), g_mem_q[l])
        mk = rmsnorm(to_heads(mk, MEM_HEADS, MEM_HD), g_mem_k[l])
        o_mem = memory_attention(mq, mk, to_heads(mv, MEM_HEADS, MEM_HD))

        gates = jax.nn.sigmoid(gate_logit.reshape(b, s, N_BRANCH, D_MODEL))
        br_sb = jnp.einsum('bse,ed->bsd', from_heads(o_sb), w_branch_sb[l])
        br_fox = jnp.einsum('bse,ed->bsd', from_heads(o_fox), w_branch_fox[l])
        br_mem = jnp.einsum('bse,ed->bsd', from_heads(o_mem), w_branch_mem[l])
        merged = gates[:, :, 0] * br_sb + gates[:, :, 1] * br_fox + gates[:, :, 2] * br_mem
        x = x + jnp.einsum('bsd,de->bse', merged, w_out[l])

        h2 = rmsnorm(x, g_mlp_norm[l])
        u = jax.nn.relu(jnp.einsum('bsd,df->bsf', h2, w_ff_up[l]))
        x = x + jnp.einsum('bsf,fd->bsd', u * u, w_ff_down[l])
    return x
```

```python
import contextlib
import os
import numpy as np
import concourse.bass as bass
import concourse.mybir as mybir
from concourse.bass_utils import run_bass_kernel_spmd

F32 = mybir.dt.float32
BF16 = mybir.dt.bfloat16
AF = mybir.ActivationFunctionType
ALU = mybir.AluOpType

S = 2048
D = 1024
DIN = 6664
MEM = 256
EPS = 1e-6
NCORES = 8
TT = S // 128
TC = S // 512
C_SBQ, C_SBK, C_SBV, C_FXQ, C_FXK, C_FXV, C_F, C_MQ, C_G = 0, 512, 1024, 1536, 2048, 2560, 3072, 3080, 3592
NEG = -30000.0

ENGS = ("pe", "act", "dve", "pool", "sp")
DMA_RING = 12
Q_DEPTH = {"sp": 8, "pool": 3}


class Op:
    __slots__ = ("eng", "emit", "deps", "signal", "is_dma", "sem", "val")

    def __init__(self, eng, emit, is_dma):
        self.eng = eng
        self.emit = emit
        self.deps = []
        self.signal = False
        self.is_dma = is_dma
        self.sem = None
        self.val = None


class Prog:
    def __init__(self, nc, es):
        self.nc = nc
        self.es = es
        self.ops = {e: [] for e in ENGS}
        self.last_w = {}
        self.readers = {}
        self.eng_sem = {e: es.enter_context(nc.semaphore(f"s_{e}")) for e in ENGS}
        self.dma_sems = {
            e: [es.enter_context(nc.semaphore(f"d_{e}{i}")) for i in range(DMA_RING)]
            for e in ("sp", "pool")
        }
        self.dma_ops = {e: [] for e in ("sp", "pool")}
        self._bar = []
        self._bar_pending = set()

    def _add(self, eng, emit, reads, writes, is_dma):
        op = Op(eng, emit, is_dma)
        deps = []
        if eng in self._bar_pending:
            self._bar_pending.discard(eng)
            for d in self._bar:
                deps.append((d, "raw"))
        for k in reads:
            w = self.last_w.get(k)
            if w is not None:
                deps.append((w, "raw"))
        for k in writes:
            w = self.last_w.get(k)
            if w is not None:
                deps.append((w, "waw"))
            for r in self.readers.get(k, ()):
                deps.append((r, "war"))
        for k in reads:
            self.readers.setdefault(k, []).append(op)
        for k in writes:
            self.last_w[k] = op
            self.readers[k] = []
        seen = set()
        for d, kind in deps:
            if d is op or id(d) in seen:
                continue
            if not d.is_dma and not is_dma and d.eng == eng:
                if eng == "pe" or kind != "raw":
                    continue
            seen.add(id(d))
            op.deps.append(d)
            d.signal = True
        if is_dma:
            q = self.dma_ops[eng]
            n = len(q)
            op.sem = self.dma_sems[eng][n % DMA_RING]
            op.val = 16 * (n // DMA_RING + 1)
            if n >= Q_DEPTH[eng]:
                op.deps.append(q[n - Q_DEPTH[eng]])
            q.append(op)
        self.ops[eng].append(op)
        return op

    def op(self, eng, emit, reads=(), writes=()):
        return self._add(eng, emit, tuple(reads), tuple(writes), False)

    def dma(self, eng, out, in_, reads=(), writes=()):
        return self._add(eng, lambda e: e.dma_start(out=out, in_=in_), tuple(reads), tuple(writes), True)

    def barrier(self):
        deps = []
        for e in ENGS:
            for o in reversed(self.ops[e]):
                if not o.is_dma:
                    deps.append(o)
                    break
        for e in ("sp", "pool"):
            deps += self.dma_ops[e][-DMA_RING:]
        self._bar = deps
        self._bar_pending = set(ENGS)

    def emit_engine(self, eng, handle):
        waited = {}
        for o in self.ops[eng]:
            for d in o.deps:
                key = d.sem.num
                if waited.get(key, 0) >= d.val:
                    continue
                handle.wait_ge(d.sem, d.val)
                waited[key] = d.val
            ins = o.emit(handle)
            if o.is_dma:
                ins.then_inc(o.sem, 16)
            elif o.signal:
                ins.then_inc(o.sem, 1)

    def run(self):
        for e in ENGS:
            cnt = 0
            for o in self.ops[e]:
                if not o.is_dma and o.signal:
                    cnt += 1
                    o.sem = self.eng_sem[e]
                    o.val = cnt
        block = self.es.enter_context(self.nc.Block())
        P = self

        @block.tensor
        def _(t):
            P.emit_engine("pe", t)

        @block.scalar
        def _(a):
            P.emit_engine("act", a)

        @block.vector
        def _(v):
            P.emit_engine("dve", v)

        @block.gpsimd
        def _(g):
            P.emit_engine("pool", g)

        @block.sync
        def _(s):
            P.emit_engine("sp", s)
            for e in ("sp", "pool"):
                for o in P.dma_ops[e][-DMA_RING:]:
                    s.wait_ge(o.sem, o.val)


def interleave(gens):
    gens = list(gens)
    while gens:
        for g in list(gens):
            try:
                next(g)
            except StopIteration:
                gens.remove(g)


def build_nc(debug=False):
    nc = bass.Bass("TRN2", target_bir_lowering=False)

    def din(name, shape):
        return nc.dram_tensor(name, shape, F32, kind="ExternalInput").ap()

    x = din("x", [S, D])
    mem = din("mem", [MEM, D])
    g_mix = din("g_mix_norm", [1, D])
    g_memn = din("g_mem_norm", [1, D])
    w_in = din("w_in", [D, DIN])
    b_forget = din("b_forget", [8, 1])
    g_fox_q = din("g_fox_q", [64, 1])
    g_fox_k = din("g_fox_k", [64, 1])
    g_mem_q = din("g_mem_q", [128, 1])
    g_mem_k = din("g_mem_k", [128, 1])
    w_mem_kv = din("w_mem_kv", [D, 1024])
    w_br = [din("w_branch_sb", [512, D]), din("w_branch_fox", [512, D]), din("w_branch_mem", [512, D])]
    w_out = din("w_out", [D, D])
    g_mlp = din("g_mlp_norm", [1, D])
    w_up = din("w_ff_up", [D, 4096])
    w_down = din("w_ff_down", [4096, D])
    consts = din("consts", [128, 7 * 128])
    out = nc.dram_tensor("out", [S, D], F32, kind="ExternalOutput").ap()
    scr = nc.dram_tensor("scr_aug", [2, 8, 3, S], BF16, kind="Internal").ap()
    dbg = {}
    if debug:
        dbg["hT"] = nc.dram_tensor("dbg_hT", [128, 8 * S], BF16, kind="ExternalOutput").ap()
        dbg["oT"] = nc.dram_tensor("dbg_oT", [128, 12 * S], BF16, kind="ExternalOutput").ap()
        dbg["mT"] = nc.dram_tensor("dbg_mT", [128, 8 * S], BF16, kind="ExternalOutput").ap()
        dbg["scr"] = nc.dram_tensor("dbg_scr", [2 * 8 * 3, S], BF16, kind="ExternalOutput").ap()
        dbg["qA"] = nc.dram_tensor("dbg_qA", [128, 512], BF16, kind="ExternalOutput").ap()
        dbg["kA"] = nc.dram_tensor("dbg_kA", [128, 512], BF16, kind="ExternalOutput").ap()
        dbg["VA"] = nc.dram_tensor("dbg_VA", [128, 512], BF16, kind="ExternalOutput").ap()

    w_in_v = w_in.rearrange("(dc p) n -> p dc n", p=128)
    w_kv_v = w_mem_kv.rearrange("(dc p) n -> p dc n", p=128)
    w_br_v = [w.rearrange("(ec p) n -> p ec n", p=128) for w in w_br]
    w_out_v = w_out.rearrange("(dc p) n -> p dc n", p=128)
    w_up_v = w_up.rearrange("(dc p) n -> p dc n", p=128)
    w_down_v = w_down.rearrange("(ft p) n -> p ft n", p=128)

    with contextlib.ExitStack() as es:
        P = Prog(nc, es)

        def sb(name, shape, dt):
            return es.enter_context(nc.sbuf_tensor(name, shape, dt))

        R1 = sb("R1", [128, 8 * S], BF16)
        R2 = sb("R2", [128, 12 * S], BF16)
        R3 = sb("R3", [128, 8 * S], BF16)
        R4 = sb("R4", [128, 8 * S], BF16)
        WT = [sb(f"WT{i}", [128, 4608], BF16) for i in range(2)]
        XT = [sb(f"XT{i}", [128, D], F32) for i in range(2)]
        GB = sb("GB", [128, D], F32)
        HB = [sb(f"HB{i}", [128, D], BF16) for i in range(2)]
        ET = [sb(f"ET{i}", [128, 512], F32) for i in range(2)]
        LP = [sb(f"LP{i}", [128, 512], BF16) for i in range(2)]
        WW = [sb(f"WW{i}", [128, 512], BF16) for i in range(2)]
        LS = [sb(f"LS{i}", [128, 512], BF16) for i in range(2)]
        RT = [sb(f"RT{i}", [128, 512], F32) for i in range(2)]
        SG = [ET[0], ET[1], RT[0]]
        MT = RT[1]
        BO = sb("BO", [128, 128], BF16)
        CST = sb("CST", [128, 7 * 128], BF16)
        ZR = sb("ZR", [128, 64], BF16)
        SS = sb("SS", [128, 40], F32)
        RS = sb("RS", [128, 40], F32)
        GQ = sb("GQ", [128, 4], F32)
        NB = sb("NB", [8, 1], F32)
        FLAST = sb("FLAST", [8, 4], F32)
        mhT = R4[:, 8192:10240].rearrange("p (a b) -> p a b", a=8)
        mkT = R4[:, 10240:11264].rearrange("p (a b) -> p a b", a=4)
        mv = R4[:, 11264:12288].rearrange("p (a b) -> p a b", a=2)
        FSP, FCS, FR, FON = ET[0][0:8, :], ET[1][0:8, :], RT[0][0:8, :], RT[1][0:8, :]
        FAP = [LP[0][0:8, :], LP[1][0:8, :], WW[0][0:8, :]]
        FAN = [WW[1][0:8, :], LS[0][0:8, :], LS[1][0:8, :]]

        PS = [es.enter_context(nc.psum_tensor(f"PS{i}", [128, 512], F32)) for i in range(7)]
        PT = es.enter_context(nc.psum_tensor("PT", [128, 8, 128], BF16))

        hT = R1[:, :].rearrange("p (a b) -> p a b", a=8)
        oT = R2[:, :].rearrange("p (a b) -> p a b", a=12)
        ident = CST[:, 0:128]
        uneg = CST[:, 128:256]
        negones = CST[:, 256:384]
        mask_sb = CST[:, 384:512]
        mask_fx = CST[:, 512:640]
        ones = CST[:, 640:768]
        blockones = BO[:, :]

        def hkeys(tc):
            return [("hT", t) for t in range(4 * tc, 4 * tc + 4)]

        ALLH = [("hT", t) for t in range(TT)]

        def pkey(ps):
            return ("PS", id(ps))

        def mm(out_, lhsT, rhs, start, stop, reads, writes):
            P.op("pe", lambda e: e.matmul(out_, lhsT, rhs, start=start, stop=stop), reads, writes)

        def wt_view(i, ncols):
            return WT[i][:, 0:8 * ncols].rearrange("p (a b) -> p a b", a=8)

        P.dma("pool", CST[:, :], consts[:, :], writes=["CST"])
        P.op("dve", lambda e: e.memset(ZR[:, :], 0.0), writes=["ZR"])
        P.op("dve", lambda e: e.memset(BO[:, :], 0.0), writes=["BO"])
        P.op("dve", lambda e: e.memset(BO[0:64, 0:64], 1.0), writes=["BO"])
        P.op("dve", lambda e: e.memset(BO[64:128, 64:128], 1.0), writes=["BO"])
        P.dma("sp", GQ[0:64, 0:1], g_fox_q[:, :], writes=["GQ"])
        P.dma("sp", GQ[64:128, 0:1], g_fox_q[:, :], writes=["GQ"])
        P.dma("sp", GQ[0:64, 1:2], g_fox_k[:, :], writes=["GQ"])
        P.dma("sp", GQ[64:128, 1:2], g_fox_k[:, :], writes=["GQ"])
        P.dma("sp", GQ[:, 2:3], g_mem_q[:, :], writes=["GQ"])
        P.dma("sp", GQ[:, 3:4], g_mem_k[:, :], writes=["GQ"])
        P.dma("sp", NB[:, :], b_forget[:, :], writes=["NB"])
        P.op("dve", lambda e: e.tensor_scalar_mul(out=GQ[:, 0:1], in0=GQ[:, 0:1], scalar1=0.125), reads=["GQ"], writes=["GQ"])
        P.op("dve", lambda e: e.tensor_scalar_mul(out=NB[:, :], in0=NB[:, :], scalar1=-1.0), reads=["NB"], writes=["NB"])

        def norm_transpose(src_rows, xt_i, col, dst, dst_keys, src_reads=(), x_loaded=False):
            xt = XT[xt_i]
            hb = HB[xt_i]
            if not x_loaded:
                P.dma("sp", xt[:, :], src_rows, reads=src_reads, writes=[("XT", xt_i)])
            P.op("act", lambda e: e.activation(out=hb[:, :], in_=xt[:, :], func=AF.Square, accum_out=SS[:, col:col + 1]),
                 reads=[("XT", xt_i)], writes=[("HB", xt_i), ("SS", col)])
            P.op("act", lambda e: e.activation(out=RS[:, col:col + 1], in_=SS[:, col:col + 1], func=AF.Ln, scale=1.0 / D, bias=EPS),
                 reads=[("SS", col)], writes=[("RS", col)])
            P.op("act", lambda e: e.activation(out=RS[:, col:col + 1], in_=RS[:, col:col + 1], func=AF.Exp, scale=-0.5),
                 reads=[("RS", col)], writes=[("RS", col)])
            P.op("dve", lambda e: e.scalar_tensor_tensor(out=hb[:, :], in0=xt[:, :], scalar=RS[:, col:col + 1], in1=GB[:, :],
                                                         op0=ALU.mult, op1=ALU.mult),
                 reads=[("XT", xt_i), ("RS", col), "GB"], writes=[("HB", xt_i)])
            for dc in range(8):
                P.op("pe", lambda e, dc=dc: e.transpose(PT[:, dc, :], hb[:, dc * 128:(dc + 1) * 128], ident),
                     reads=[("HB", xt_i), "CST"], writes=["PT"])
            P.op("act", lambda e: e.activation(out=dst, in_=PT[:, :, :], func=AF.Copy), reads=["PT"], writes=dst_keys)

        P.dma("sp", GB[:, :], g_mix[0:1, :].partition_broadcast(128), writes=["GB"])
        for tt in range(TT):
            norm_transpose(x[tt * 128:(tt + 1) * 128, :], tt % 2, tt, hT[:, :, tt * 128:(tt + 1) * 128], [("hT", tt)])
        if debug:
            P.dma("sp", dbg["hT"][:, :], R1[:, :], reads=ALLH)

        WF = WT[1][:, 4096:4160].rearrange("p (a b) -> p a b", a=8)
        P.op("dve", lambda e: e.memset(FON, 1.0), writes=["FON"])
        P.dma("pool", WF, w_in_v[:, :, C_F:C_F + 8], writes=[("WF",)])
        for tc in range(TC):
            ps = PS[tc % 2]
            for dc in range(8):
                mm(ps[0:8, :], WF[:, dc, :], hT[:, dc, tc * 512:(tc + 1) * 512], dc == 0, dc == 7, [("WF",)] + hkeys(tc), [pkey(ps)])
            P.op("act", lambda e, ps=ps: e.activation(out=FSP, in_=ps[0:8, :], func=AF.Exp, scale=-1.0, bias=NB[:, 0:1]),
                 reads=[pkey(ps), "NB"], writes=["FSP"])
            P.op("act", lambda e: e.activation(out=FSP, in_=FSP, func=AF.Ln, bias=1.0), reads=["FSP"], writes=["FSP"])
            if tc == 0:
                P.op("dve", lambda e: e.tensor_tensor_scan(out=FCS, data0=FON, data1=FSP, initial=0.0,
                                                           op0=ALU.mult, op1=ALU.add), reads=["FON", "FSP"], writes=["FCS"])
            else:
                P.op("dve", lambda e, tc=tc: e.tensor_tensor_scan(out=FCS, data0=FON, data1=FSP,
                                                                  initial=FLAST[:, tc - 1:tc], op0=ALU.mult, op1=ALU.add),
                     reads=["FON", "FSP", ("FLAST", tc - 1)], writes=["FCS"])
            P.op("dve", lambda e, tc=tc: e.tensor_copy(out=FLAST[:, tc:tc + 1], in_=FCS[:, 511:512]), reads=["FCS"], writes=[("FLAST", tc)])
            P.op("dve", lambda e: e.tensor_copy(out=FAP[0], in_=FCS), reads=["FCS"], writes=["FAP0"])
            P.op("dve", lambda e: e.tensor_tensor(out=FR, in0=FCS, in1=FAP[0], op=ALU.subtract), reads=["FCS", "FAP0"], writes=["FR"])
            P.op("dve", lambda e: e.tensor_copy(out=FAP[1], in_=FR), reads=["FR"], writes=["FAP1"])
            P.op("dve", lambda e: e.tensor_tensor(out=FR, in0=FR, in1=FAP[1], op=ALU.subtract), reads=["FR", "FAP1"], writes=["FR"])
            P.op("dve", lambda e: e.tensor_copy(out=FAP[2], in_=FR), reads=["FR"], writes=["FAP2"])
            for k in range(3):
                P.op("dve", lambda e, k=k: e.tensor_scalar_mul(out=FAN[k], in0=FAP[k], scalar1=-1.0), reads=[f"FAP{k}"], writes=[f"FAN{k}"])
                P.dma("sp", scr[0, :, k, tc * 512:(tc + 1) * 512], FAP[k], reads=[f"FAP{k}"], writes=["scr"])
                P.dma("sp", scr[1, :, k, tc * 512:(tc + 1) * 512], FAN[k], reads=[f"FAN{k}"], writes=["scr"])

        P.barrier()

        def proj_fm(ps, wv, fcol, tc, wkey):
            for dc in range(8):
                mm(ps[:, :], wv[:, dc, fcol * 128:(fcol + 1) * 128], hT[:, dc, tc * 512:(tc + 1) * 512],
                   dc == 0, dc == 7, [wkey] + hkeys(tc), [("PS", id(ps))])

        qT = R3[:, 0:4 * S].rearrange("p (a b) -> p a b", a=4)
        kT = R3[:, 4 * S:8 * S].rearrange("p (a b) -> p a b", a=4)
        vS = R4[:, 0:16 * 512].rearrange("p (a b) -> p a b", a=16)
        for which, c0, dst, scale in ((0, C_SBQ, qT, 0.125), (1, C_SBK, kT, 1.0)):
            P.dma("pool", wt_view(which, 512), w_in_v[:, :, c0:c0 + 512], writes=[("WT", which)])
            for ft in range(4):
                for tc in range(TC):
                    ps = PS[(ft * 4 + tc) % 2]
                    proj_fm(ps, wt_view(which, 512), ft, tc, ("WT", which))
                    P.op("act", lambda e, ps=ps, dst=dst, ft=ft, tc=tc, scale=scale:
                         e.activation(out=dst[:, ft, tc * 512:(tc + 1) * 512], in_=ps[:, :], func=AF.Copy, scale=scale),
                         reads=[pkey(ps)], writes=[("qk", which, ft, tc)])
        P.dma("pool", wt_view(0, 512), w_in_v[:, :, C_SBV:C_SBV + 512], writes=[("WT", 0)])
        for tt in range(TT):
            ps = PS[tt % 2]
            for dc in range(8):
                mm(ps[:, :], hT[:, dc, tt * 128:(tt + 1) * 128], wt_view(0, 512)[:, dc, :], dc == 0, dc == 7,
                   [("WT", 0), ("hT", tt)], [pkey(ps)])
            P.op("dve", lambda e, ps=ps, tt=tt: e.tensor_copy(out=vS[:, tt, :], in_=ps[:, :]), reads=[pkey(ps)], writes=[("v", tt)])

        def sb_stream(h, s):
            ft, pb = h // 2, 64 * (h % 2)
            A, O = PS[2 + s], PS[4 + s]
            E_, Lp, W_, Ls = ET[s], LP[s], WW[s], LS[s]
            kA, kO, kE, kL, kW, kS = pkey(A), pkey(O), ("ET", s), ("LP", s), ("WW", s), ("LS", s)
            for c in range(TC):
                qk_r = [("qk", 0, ft, c)]
                mm(O[0:64, :], ZR[:, 0:64], hT[:, 0, 0:512], True, False, ["ZR"] + hkeys(0), [kO])
                P.op("pool", lambda e: e.memset(Ls[:, :], 0.0), writes=[kS])
                yield
                top = 4 * c + 3
                for kb in range(top, -1, -1):
                    j = kb - 4 * c
                    q0 = 128 * j if j > 0 else 0
                    kq_r = [("qk", 1, ft, kb // 4)]
                    mm(A[:, q0:512], kT[pb:pb + 64, ft, kb * 128:(kb + 1) * 128],
                       qT[pb:pb + 64, ft, c * 512 + q0:(c + 1) * 512], True, j < 0, qk_r + kq_r, [kA])
                    if j >= 0:
                        mm(A[:, q0:q0 + 128], ident, mask_sb, False, True, ["CST"], [kA])
                    yield
                    P.op("act", lambda e, q0=q0: e.activation(out=E_[:, q0:512], in_=A[:, q0:512], func=AF.Exp),
                         reads=[kA], writes=[kE])
                    yield
                    P.op("act", lambda e, q0=q0: e.activation(out=Lp[:, q0:512], in_=E_[:, q0:512], func=AF.Ln, bias=1.0),
                         reads=[kE], writes=[kL])
                    yield
                    mm(A[:, q0:512], uneg, Lp[:, q0:512], False, kb == top, ["CST", kL], [kA])
                    if kb != top:
                        mm(A[:, q0:512], negones, Ls[:, q0:512], False, True, ["CST", kS], [kA])
                    yield
                    P.op("act", lambda e, q0=q0: e.activation(out=W_[:, q0:512], in_=A[:, q0:512], func=AF.Exp),
                         reads=[kA], writes=[kW])
                    if kb != 0:
                        P.op("dve", lambda e, q0=q0: e.tensor_tensor(out=Ls[:, q0:512], in0=Ls[:, q0:512], in1=Lp[:, q0:512], op=ALU.add),
                             reads=[kS, kL], writes=[kS])
                    yield
                    mm(O[0:64, q0:512], vS[:, kb, h * 64:(h + 1) * 64], W_[:, q0:512], False, kb == 0, [("v", kb), kW], [kO])
                    yield
                P.op("dve", lambda e, c=c: e.tensor_copy(out=oT[pb:pb + 64, ft, c * 512:(c + 1) * 512], in_=O[0:64, :]),
                     reads=[kO], writes=[("oT", ft, c, pb)])
                yield

        for hp in range(4):
            interleave([sb_stream(2 * hp, 0), sb_stream(2 * hp + 1, 1)])

        P.barrier()
        qA = R3[:, 0:4 * S].rearrange("p (a b) -> p a b", a=4)
        kA_ = R3[:, 4 * S:8 * S].rearrange("p (a b) -> p a b", a=4)
        VA = R4[:, 0:16 * 512].rearrange("p (a b c) -> p a b c", a=16, b=4)
        for half in range(2):
            for t_, name in ((qA, "qA"), (kA_, "kA")):
                P.op("pool", lambda e, t_=t_: e.memset(t_[64:128, :, :], 0.0), reads=[], writes=[(name, "aug")] + [(name, hl, tc) for hl in range(4) for tc in range(TC)])
                P.op("pool", lambda e, t_=t_: e.memset(t_[64:70, :, :], 1.0), reads=[], writes=[(name, "aug")])
            for hl in range(4):
                h = half * 4 + hl
                P.dma("sp", qA[64:67, hl, :], scr[1, h, :, :], reads=["scr"], writes=[("qA", "aug")])
                P.dma("sp", kA_[67:70, hl, :], scr[0, h, :, :], reads=["scr"], writes=[("kA", "aug")])
            if half == 0:
                P.op("pool", lambda e: e.memset(VA[:, :, :, 64:128], 1.0), reads=[], writes=[("VA", "ones")])
            for which, c0, dst, dname in ((0, C_FXQ, qA, "qA"), (1, C_FXK, kA_, "kA")):
                P.dma("pool", wt_view(which, 256), w_in_v[:, :, c0 + half * 256:c0 + half * 256 + 256], writes=[("WT", which)])
                for fl in range(2):
                    for tc in range(TC):
                        ps, ps2 = PS[0], PS[1]
                        proj_fm(ps, wt_view(which, 256), fl, tc, ("WT", which))
                        P.op("act", lambda e, ps=ps: e.activation(out=LP[0][:, :], in_=ps[:, :], func=AF.Square), reads=[pkey(ps)], writes=[("LP", 0)])
                        mm(ps2[:, :], blockones, LP[0][:, :], True, True, ["BO", ("LP", 0)], [pkey(ps2)])
                        for odd in range(2):
                            hl = 2 * fl + odd
                            pb = 64 * odd
                            rt = RT[odd]
                            P.op("act", lambda e, rt=rt, pb=pb, ps2=ps2: e.activation(out=rt[0:64, :], in_=ps2[pb:pb + 64, :], func=AF.Ln, scale=1.0 / 64, bias=EPS),
                                 reads=[pkey(ps2)], writes=[("RT", odd)])
                            P.op("act", lambda e, rt=rt: e.activation(out=rt[0:64, :], in_=rt[0:64, :], func=AF.Exp, scale=-0.5),
                                 reads=[("RT", odd)], writes=[("RT", odd)])
                            if odd:
                                P.op("act", lambda e, ps=ps: e.activation(out=ET[0][0:64, :], in_=ps[64:128, :], func=AF.Copy),
                                     reads=[pkey(ps)], writes=[("ET", 0)])
                                src, skey = ET[0][0:64, :], ("ET", 0)
                            else:
                                src, skey = ps[0:64, :], pkey(ps)
                            P.op("dve", lambda e, hl=hl, tc=tc, dst=dst, which=which, rt=rt, src=src:
                                 e.scalar_tensor_tensor(out=dst[0:64, hl, tc * 512:(tc + 1) * 512], in0=src,
                                                        scalar=GQ[0:64, which:which + 1], in1=rt[0:64, :],
                                                        op0=ALU.mult, op1=ALU.mult),
                                 reads=[skey, "GQ", ("RT", odd)], writes=[(dname, hl, tc)])
            P.dma("pool", wt_view(0, 256), w_in_v[:, :, C_FXV + half * 256:C_FXV + half * 256 + 256], writes=[("WT", 0)])
            for tt in range(TT):
                ps = PS[tt % 2]
                for dc in range(8):
                    mm(ps[:, 0:256], hT[:, dc, tt * 128:(tt + 1) * 128], wt_view(0, 256)[:, dc, :], dc == 0, dc == 7,
                       [("WT", 0), ("hT", tt)], [pkey(ps)])
                P.op("dve", lambda e, ps=ps, tt=tt: e.tensor_copy(out=VA[:, tt, :, 0:64], in_=ps[:, 0:256].rearrange("p (a b) -> p a b", a=4)),
                     reads=[pkey(ps)], writes=[("VA", tt)])

            def fox_stream(hl, s, half=half):
                h = half * 4 + hl
                ft, pb = h // 2, 64 * (h % 2)
                A, O = PS[2 + s], PS[4 + s]
                W_ = WW[s]
                kA, kO, kW, kR = pkey(A), pkey(O), ("WW", s), ("RT", s)
                for c in range(TC):
                    top = 4 * c + 3
                    for kb in range(0, top + 1):
                        j = kb - 4 * c
                        q0 = 128 * j if j > 0 else 0
                        mm(A[:, q0:512], kA_[:, hl, kb * 128:(kb + 1) * 128], qA[:, hl, c * 512 + q0:(c + 1) * 512], True, j < 0,
                           [("kA", hl, kb // 4), ("kA", "aug"), ("qA", hl, c), ("qA", "aug")], [kA])
                        if j >= 0:
                            mm(A[:, q0:q0 + 128], ident, mask_fx, False, True, ["CST"], [kA])
                        yield
                        P.op("act", lambda e, q0=q0: e.activation(out=W_[:, q0:512], in_=A[:, q0:512], func=AF.Exp), reads=[kA], writes=[kW])
                        yield
                        mm(O[:, q0:512], VA[:, kb, hl, :], W_[:, q0:512], kb == 0, kb == top, [("VA", kb), ("VA", "ones"), kW], [kO])
                        yield
                    P.op("act", lambda e: e.activation(out=RT[s][pb:pb + 64, :], in_=O[64:128, :], func=AF.Copy), reads=[kO], writes=[kR])
                    yield
                    P.op("dve", lambda e: e.reciprocal(out=RT[s][pb:pb + 64, :], in_=RT[s][pb:pb + 64, :]), reads=[kR], writes=[kR])
                    yield
                    P.op("dve", lambda e, c=c: e.tensor_tensor(out=oT[pb:pb + 64, 4 + ft, c * 512:(c + 1) * 512], in0=O[0:64, :],
                                                               in1=RT[s][pb:pb + 64, :], op=ALU.mult),
                         reads=[kO, kR], writes=[("oT", 4 + ft, c, pb)])
                    yield

            for hp in range(2):
                interleave([fox_stream(2 * hp, 0), fox_stream(2 * hp + 1, 1)])

        if debug:
            P.barrier()
            P.dma("sp", dbg["scr"][:, :], scr.rearrange("a b c s -> (a b c) s"), reads=[])
            P.dma("sp", dbg["qA"][:, :], qA[:, 0, 0:512], reads=[])
            P.dma("sp", dbg["kA"][:, :], kA_[:, 0, 0:512], reads=[])
            P.dma("sp", dbg["VA"][:, :], R4[:, 0:512], reads=[])
        P.barrier()
        P.dma("sp", GB[:, :], g_memn[0:1, :].partition_broadcast(128), reads=[], writes=["GB"])
        for mt in range(2):
            norm_transpose(mem[mt * 128:(mt + 1) * 128, :], mt, 16 + mt, mhT[:, :, mt * 128:(mt + 1) * 128], [("mhT", mt)])
        MH = [("mhT", 0), ("mhT", 1)]

        def head_norm128(ps, ps2, gcol, dst, dkeys, n):
            P.op("act", lambda e: e.activation(out=LP[0][:, 0:n], in_=ps[:, 0:n], func=AF.Square), reads=[pkey(ps)], writes=[("LP", 0)])
            mm(ps2[:, 0:n], ones, LP[0][:, 0:n], True, True, ["CST", ("LP", 0)], [pkey(ps2)])
            P.op("act", lambda e: e.activation(out=RT[0][:, 0:n], in_=ps2[:, 0:n], func=AF.Ln, scale=1.0 / 128, bias=EPS),
                 reads=[pkey(ps2)], writes=[("RT", 0)])
            P.op("act", lambda e: e.activation(out=RT[0][:, 0:n], in_=RT[0][:, 0:n], func=AF.Exp, scale=-0.5), reads=[("RT", 0)], writes=[("RT", 0)])
            P.op("dve", lambda e: e.scalar_tensor_tensor(out=dst, in0=ps[:, 0:n], scalar=GQ[:, gcol:gcol + 1], in1=RT[0][:, 0:n],
                                                         op0=ALU.mult, op1=ALU.mult),
                 reads=[pkey(ps), "GQ", ("RT", 0)], writes=dkeys)

        P.dma("pool", wt_view(0, 512), w_kv_v[:, :, 0:512], writes=[("WT", 0)])
        P.dma("pool", wt_view(1, 512), w_kv_v[:, :, 512:1024], writes=[("WT", 1)])
        for hd in range(4):
            ps, ps2 = PS[0], PS[1]
            for dc in range(8):
                mm(ps[:, 0:MEM], wt_view(0, 512)[:, dc, hd * 128:(hd + 1) * 128], mhT[:, dc, :], dc == 0, dc == 7, [("WT", 0)] + MH, [pkey(ps)])
            head_norm128(ps, ps2, 3, mkT[:, hd, :], [("mkT", hd)], MEM)
        for mt in range(2):
            ps = PS[mt]
            for dc in range(8):
                mm(ps[:, :], mhT[:, dc, mt * 128:(mt + 1) * 128], wt_view(1, 512)[:, dc, :], dc == 0, dc == 7, [("WT", 1)] + MH, [pkey(ps)])
            P.op("dve", lambda e, ps=ps, mt=mt: e.tensor_copy(out=mv[:, mt, :], in_=ps[:, :]), reads=[pkey(ps)], writes=[("mv", mt)])
        mqT = R3[:, 0:4 * S].rearrange("p (a b) -> p a b", a=4)
        P.dma("pool", wt_view(0, 512), w_in_v[:, :, C_MQ:C_MQ + 512], writes=[("WT", 0)])
        for hd in range(4):
            for tc in range(TC):
                ps, ps2 = PS[0], PS[1]
                proj_fm(ps, wt_view(0, 512), hd, tc, ("WT", 0))
                head_norm128(ps, ps2, 2, mqT[:, hd, tc * 512:(tc + 1) * 512], [("mqT", hd, tc)], 512)

        MSCALE = 128.0 ** -0.5

        def mem_stream(hd, s):
            A, O, Dn = PS[s], PS[2 + s], PS[4 + s]
            W_ = WW[s]
            kA, kO, kD, kW, kR = pkey(A), pkey(O), pkey(Dn), ("WW", s), ("RT", s)
            for c in range(TC):
                for mt in range(2):
                    mm(A[:, :], mkT[:, hd, mt * 128:(mt + 1) * 128], mqT[:, hd, c * 512:(c + 1) * 512], True, True,
                       [("mkT", hd), ("mqT", hd, c)], [kA])
                    yield
                    P.op("act", lambda e: e.activation(out=W_[:, :], in_=A[:, :], func=AF.Exp, scale=MSCALE), reads=[kA], writes=[kW])
                    yield
                    mm(O[:, :], mv[:, mt, hd * 128:(hd + 1) * 128], W_[:, :], mt == 0, mt == 1, [("mv", mt), kW], [kO])
                    mm(Dn[:, :], ones, W_[:, :], mt == 0, mt == 1, ["CST", kW], [kD])
                    yield
                P.op("dve", lambda e: e.reciprocal(out=RT[s][:, :], in_=Dn[:, :]), reads=[kD], writes=[kR])
                yield
                P.op("dve", lambda e, c=c: e.tensor_tensor(out=oT[:, 8 + hd, c * 512:(c + 1) * 512], in0=O[:, :], in1=RT[s][:, :], op=ALU.mult),
                     reads=[kO, kR], writes=[("oT", 8 + hd, c, 0)])
                yield

        for hp in range(2):
            interleave([mem_stream(2 * hp, 0), mem_stream(2 * hp + 1, 1)])

        if debug:
            P.barrier()
            P.dma("sp", dbg["oT"][:, :], R2[:, :], reads=[])

        P.barrier()
        mT = R3[:, :].rearrange("p (a b) -> p a b", a=8)
        for j in range(8):
            wi = j % 2
            wg = WT[wi][:, 0:3072].rearrange("p (b a c) -> p b a c", b=3, a=8)
            wb = WT[wi][:, 3072:4608].rearrange("p (b a c) -> p b a c", b=3, a=4)
            for b in range(3):
                P.dma("pool", wg[:, b, :, :], w_in_v[:, :, C_G + b * 1024 + j * 128:C_G + b * 1024 + (j + 1) * 128], writes=[("WT", wi)])
                P.dma("pool", wb[:, b, :, :], w_br_v[b][:, :, j * 128:(j + 1) * 128], writes=[("WT", wi)])
            for c in range(TC):
                for b in range(3):
                    G, BR = PS[b], PS[3 + b]
                    for dc in range(8):
                        mm(G[:, :], wg[:, b, dc, :], hT[:, dc, c * 512:(c + 1) * 512], dc == 0, dc == 7, [("WT", wi)] + hkeys(c), [pkey(G)])
                    for ec in range(4):
                        mm(BR[:, :], wb[:, b, ec, :], oT[:, 4 * b + ec, c * 512:(c + 1) * 512], ec == 0, ec == 3, [("WT", wi)], [pkey(BR)])
                    P.op("act", lambda e, b=b, G=G: e.activation(out=SG[b][:, :], in_=G[:, :], func=AF.Sigmoid), reads=[pkey(G)], writes=[("SG", b)])
                    if b == 0:
                        P.op("dve", lambda e, BR=BR: e.tensor_tensor(out=MT[:, :], in0=BR[:, :], in1=SG[0][:, :], op=ALU.mult),
                             reads=[pkey(BR), ("SG", 0)], writes=["MT"])
                    else:
                        P.op("dve", lambda e, b=b, BR=BR: e.tensor_tensor(out=SG[b][:, :], in0=BR[:, :], in1=SG[b][:, :], op=ALU.mult),
                             reads=[pkey(BR), ("SG", b)], writes=[("SG", b)])
                        if b == 1:
                            P.op("pool", lambda e: e.tensor_tensor(out=MT[:, :], in0=MT[:, :], in1=SG[1][:, :], op=ALU.add),
                                 reads=["MT", ("SG", 1)], writes=["MT"])
                        else:
                            P.op("pool", lambda e, j=j, c=c: e.tensor_tensor(out=mT[:, j, c * 512:(c + 1) * 512], in0=MT[:, :], in1=SG[2][:, :], op=ALU.add),
                                 reads=["MT", ("SG", 2)], writes=[("mT", j, c)])
        if debug:
            P.barrier()
            P.dma("sp", dbg["mT"][:, :], R3[:, :], reads=[])

        P.barrier()
        WO = R4[:, 0:8 * D].rearrange("p (a b) -> p a b", a=8)
        WD_A = R2[:, 0:24 * D].rearrange("p (a b) -> p a b", a=24)
        WD_B = R4[:, 8 * D:16 * D].rearrange("p (a b) -> p a b", a=8)
        h2T = hT

        def wdown(ft):
            return WD_A[:, ft, :] if ft < 24 else WD_B[:, ft - 24, :]

        for hf in range(2):
            P.dma("pool", WO[:, :, hf * 512:(hf + 1) * 512], w_out_v[:, :, hf * 512:(hf + 1) * 512], writes=[("WO", hf)])
        P.dma("sp", GB[:, :], g_mlp[0:1, :].partition_broadcast(128), reads=[], writes=["GB"])
        for tt in range(TT):
            xi = tt % 2
            P.dma("sp", XT[xi][:, :], x[tt * 128:(tt + 1) * 128, :], writes=[("XT", xi)])
            for hf in range(2):
                ps = PS[hf]
                for dc in range(8):
                    mm(ps[:, :], mT[:, dc, tt * 128:(tt + 1) * 128], WO[:, dc, hf * 512:(hf + 1) * 512], dc == 0, dc == 7,
                       [("WO", hf), ("mT", dc, tt // 4)], [pkey(ps)])
                P.op("dve", lambda e, ps=ps, xi=xi, hf=hf: e.tensor_tensor(out=XT[xi][:, hf * 512:(hf + 1) * 512], in0=ps[:, :],
                                                                             in1=XT[xi][:, hf * 512:(hf + 1) * 512], op=ALU.add),
                     reads=[pkey(ps), ("XT", xi)], writes=[("XT", xi)])
            P.dma("sp", out[tt * 128:(tt + 1) * 128, :], XT[xi][:, :], reads=[("XT", xi)], writes=[("out", tt)])
            norm_transpose(None, xi, 20 + tt, h2T[:, :, tt * 128:(tt + 1) * 128], [("hT", tt)], x_loaded=True)
            if tt == 3:
                for g in range(8):
                    dstv = WD_A[:, 4 * g:4 * g + 4, :] if g < 6 else WD_B[:, 4 * (g - 6):4 * (g - 6) + 4, :]
                    P.dma("pool", dstv, w_down_v[:, 4 * g:4 * g + 4, :], writes=[("WD", g)])

        P.barrier()
        u2 = R3[:, 0:32 * 512].rearrange("p (a b) -> p a b", a=32)
        for c in range(TC):
            for fq in range(8):
                wi = fq % 2
                P.dma("pool", wt_view(wi, 512), w_up_v[:, :, fq * 512:(fq + 1) * 512], writes=[("WT", wi)])
                for fl in range(4):
                    ft = fq * 4 + fl
                    ps = PS[ft % 2]
                    rl = LP[ft % 2]
                    for dc in range(8):
                        mm(ps[:, :], wt_view(wi, 512)[:, dc, fl * 128:(fl + 1) * 128], h2T[:, dc, c * 512:(c + 1) * 512], dc == 0, dc == 7,
                           [("WT", wi)] + hkeys(c), [pkey(ps)])
                    P.op("act", lambda e, ps=ps, rl=rl: e.activation(out=rl[:, :], in_=ps[:, :], func=AF.Relu), reads=[pkey(ps)], writes=[("LP", ft % 2)])
                    P.op("pool", lambda e, rl=rl, ft=ft: e.tensor_tensor(out=u2[:, ft, :], in0=rl[:, :], in1=rl[:, :], op=ALU.mult),
                         reads=[("LP", ft % 2)], writes=[("u2", ft)])
            for hf in range(2):
                for tl in range(4):
                    tt = 4 * c + tl
                    ps = PS[2 + tl]
                    xi = tl % 2
                    P.dma("sp", XT[xi][:, 0:512], out[tt * 128:(tt + 1) * 128, hf * 512:(hf + 1) * 512], reads=[("out", tt)], writes=[("XT", xi)])
                    for ft in range(32):
                        mm(ps[:, :], u2[:, ft, tl * 128:(tl + 1) * 128], wdown(ft)[:, hf * 512:(hf + 1) * 512], ft == 0, ft == 31,
                           [("u2", ft), ("WD", ft // 4)], [pkey(ps)])
                    P.op("dve", lambda e, ps=ps, xi=xi: e.tensor_tensor(out=XT[xi][:, 0:512], in0=ps[:, :], in1=XT[xi][:, 0:512], op=ALU.add),
                         reads=[pkey(ps), ("XT", xi)], writes=[("XT", xi)])
                    P.dma("sp", out[tt * 128:(tt + 1) * 128, hf * 512:(hf + 1) * 512], XT[xi][:, 0:512], reads=[("XT", xi)], writes=[("out", tt)])

        P.run()
    return nc


def make_consts():
    i = np.arange(128)
    ident = np.eye(128, dtype=np.float32)
    uneg = -(i[:, None] >= i[None, :]).astype(np.float32)
    negones = -np.ones((128, 128), np.float32)
    mask_sb = np.where(i[:, None] >= i[None, :], NEG, 0.0).astype(np.float32)
    mask_fx = np.where(i[:, None] > i[None, :], NEG, 0.0).astype(np.float32)
    ones = np.ones((128, 128), np.float32)
    blk = np.zeros((128, 128), np.float32)
    blk[:64, :64] = 1.0
    blk[64:, 64:] = 1.0
    return np.concatenate([ident, uneg, negones, mask_sb, mask_fx, ones, blk], axis=1)


_NC_CACHE = {}


def kernel(x, mem, g_mix_norm, g_mem_norm, w_in, b_forget, g_fox_q, g_fox_k, g_mem_q, g_mem_k,
           w_mem_kv, w_branch_sb, w_branch_fox, w_branch_mem, w_out, g_mlp_norm, w_ff_up, w_ff_down):
    debug = bool(os.environ.get("MK_DEBUG"))
    f = lambda a: np.ascontiguousarray(np.asarray(a, dtype=np.float32))
    x = f(x)
    mem = f(mem)
    shared = {
        "g_mix_norm": f(g_mix_norm)[0].reshape(1, D),
        "g_mem_norm": f(g_mem_norm)[0].reshape(1, D),
        "w_in": f(w_in)[0],
        "b_forget": f(b_forget)[0].reshape(8, 1),
        "g_fox_q": f(g_fox_q)[0].reshape(64, 1),
        "g_fox_k": f(g_fox_k)[0].reshape(64, 1),
        "g_mem_q": f(g_mem_q)[0].reshape(128, 1),
        "g_mem_k": f(g_mem_k)[0].reshape(128, 1),
        "w_mem_kv": f(w_mem_kv)[0],
        "w_branch_sb": f(w_branch_sb)[0],
        "w_branch_fox": f(w_branch_fox)[0],
        "w_branch_mem": f(w_branch_mem)[0],
        "w_out": f(w_out)[0],
        "g_mlp_norm": f(g_mlp_norm)[0].reshape(1, D),
        "w_ff_up": f(w_ff_up)[0],
        "w_ff_down": f(w_ff_down)[0],
        "consts": make_consts(),
    }
    nc = build_nc(debug)
    ncores = int(os.environ.get("MK_CORES", NCORES)) if debug else NCORES
    in_maps = [dict(shared, x=x[b], mem=mem[b]) for b in range(ncores)]
    res = run_bass_kernel_spmd(nc, in_maps, core_ids=list(range(ncores)))
    if debug:
        kernel.debug = res.results
    return np.stack([np.asarray(r["out"], dtype=np.float32) for r in res.results], axis=0)
```

```python
import contextlib
import os
import numpy as np
import concourse.bass as bass
import concourse.mybir as mybir
from concourse.bass_utils import run_bass_kernel_spmd

F32 = mybir.dt.float32
BF16 = mybir.dt.bfloat16
AF = mybir.ActivationFunctionType
ALU = mybir.AluOpType

S = 2048
D = 1024
DIN = 6664
MEM = 256
EPS = 1e-6
NCORES = 8
TT = S // 128
TC = S // 512
C_SBQ, C_SBK, C_SBV, C_FXQ, C_FXK, C_FXV, C_F, C_MQ, C_G = 0, 512, 1024, 1536, 2048, 2560, 3072, 3080, 3592
NEG = -30000.0

ENGS = ("pe", "act", "dve", "pool", "sp")
DMA_RING = 12
Q_DEPTH = {"sp": 8, "pool": 3}


class Op:
    __slots__ = ("eng", "emit", "deps", "signal", "is_dma", "sem", "val")

    def __init__(self, eng, emit, is_dma):
        self.eng = eng
        self.emit = emit
        self.deps = []
        self.signal = False
        self.is_dma = is_dma
        self.sem = None
        self.val = None


class Prog:
    def __init__(self, nc, es):
        self.nc = nc
        self.es = es
        self.ops = {e: [] for e in ENGS}
        self.last_w = {}
        self.readers = {}
        self.eng_sem = {e: es.enter_context(nc.semaphore(f"s_{e}")) for e in ENGS}
        self.dma_sems = {
            e: [es.enter_context(nc.semaphore(f"d_{e}{i}")) for i in range(DMA_RING)]
            for e in ("sp", "pool")
        }
        self.dma_ops = {e: [] for e in ("sp", "pool")}
        self._bar = []
        self._bar_pending = set()

    def _add(self, eng, emit, reads, writes, is_dma):
        op = Op(eng, emit, is_dma)
        deps = []
        if eng in self._bar_pending:
            self._bar_pending.discard(eng)
            for d in self._bar:
                deps.append((d, "raw"))
        for k in reads:
            w = self.last_w.get(k)
            if w is not None:
                deps.append((w, "raw"))
        for k in writes:
            w = self.last_w.get(k)
            if w is not None:
                deps.append((w, "waw"))
            for r in self.readers.get(k, ()):
                deps.append((r, "war"))
        for k in reads:
            self.readers.setdefault(k, []).append(op)
        for k in writes:
            self.last_w[k] = op
            self.readers[k] = []
        seen = set()
        for d, kind in deps:
            if d is op or id(d) in seen:
                continue
            if not d.is_dma and not is_dma and d.eng == eng:
                if eng == "pe" or kind != "raw":
                    continue
            seen.add(id(d))
            op.deps.append(d)
            d.signal = True
        if is_dma:
            q = self.dma_ops[eng]
            n = len(q)
            op.sem = self.dma_sems[eng][n % DMA_RING]
            op.val = 16 * (n // DMA_RING + 1)
            if n >= Q_DEPTH[eng]:
                op.deps.append(q[n - Q_DEPTH[eng]])
            q.append(op)
        self.ops[eng].append(op)
        return op

    def op(self, eng, emit, reads=(), writes=()):
        return self._add(eng, emit, tuple(reads), tuple(writes), False)

    def dma(self, eng, out, in_, reads=(), writes=()):
        return self._add(eng, lambda e: e.dma_start(out=out, in_=in_), tuple(reads), tuple(writes), True)

    def barrier(self):
        deps = []
        for e in ENGS:
            for o in reversed(self.ops[e]):
                if not o.is_dma:
                    deps.append(o)
                    break
        for e in ("sp", "pool"):
            deps += self.dma_ops[e][-DMA_RING:]
        self._bar = deps
        self._bar_pending = set(ENGS)

    def emit_engine(self, eng, handle):
        waited = {}
        for o in self.ops[eng]:
            for d in o.deps:
                key = d.sem.num
                if waited.get(key, 0) >= d.val:
                    continue
                handle.wait_ge(d.sem, d.val)
                waited[key] = d.val
            ins = o.emit(handle)
            if o.is_dma:
                ins.then_inc(o.sem, 16)
            elif o.signal:
                ins.then_inc(o.sem, 1)

    def run(self):
        for e in ENGS:
            cnt = 0
            for o in self.ops[e]:
                if not o.is_dma and o.signal:
                    cnt += 1
                    o.sem = self.eng_sem[e]
                    o.val = cnt
        block = self.es.enter_context(self.nc.Block())
        P = self

        @block.tensor
        def _(t):
            P.emit_engine("pe", t)

        @block.scalar
        def _(a):
            P.emit_engine("act", a)

        @block.vector
        def _(v):
            P.emit_engine("dve", v)

        @block.gpsimd
        def _(g):
            P.emit_engine("pool", g)

        @block.sync
        def _(s):
            P.emit_engine("sp", s)
            for e in ("sp", "pool"):
                for o in P.dma_ops[e][-DMA_RING:]:
                    s.wait_ge(o.sem, o.val)


def interleave(gens):
    gens = list(gens)
    while gens:
        for g in list(gens):
            try:
                next(g)
            except StopIteration:
                gens.remove(g)


def build_nc(debug=False):
    nc = bass.Bass("TRN2", target_bir_lowering=False)

    def din(name, shape):
        return nc.dram_tensor(name, shape, F32, kind="ExternalInput").ap()

    x = din("x", [S, D])
    mem = din("mem", [MEM, D])
    g_mix = din("g_mix_norm", [1, D])
    g_memn = din("g_mem_norm", [1, D])
    w_in = din("w_in", [D, DIN])
    b_forget = din("b_forget", [8, 1])
    g_fox_q = din("g_fox_q", [64, 1])
    g_fox_k = din("g_fox_k", [64, 1])
    g_mem_q = din("g_mem_q", [128, 1])
    g_mem_k = din("g_mem_k", [128, 1])
    w_mem_kv = din("w_mem_kv", [D, 1024])
    w_br = [din("w_branch_sb", [512, D]), din("w_branch_fox", [512, D]), din("w_branch_mem", [512, D])]
    w_out = din("w_out", [D, D])
    g_mlp = din("g_mlp_norm", [1, D])
    w_up = din("w_ff_up", [D, 4096])
    w_down = din("w_ff_down", [4096, D])
    consts = din("consts", [128, 7 * 128])
    out = nc.dram_tensor("out", [S, D], F32, kind="ExternalOutput").ap()
    scr = nc.dram_tensor("scr_aug", [2, 8, 3, S], BF16, kind="Internal").ap()
    dbg = {}
    if debug:
        dbg["hT"] = nc.dram_tensor("dbg_hT", [128, 8 * S], BF16, kind="ExternalOutput").ap()
        dbg["oT"] = nc.dram_tensor("dbg_oT", [128, 12 * S], BF16, kind="ExternalOutput").ap()
        dbg["mT"] = nc.dram_tensor("dbg_mT", [128, 8 * S], BF16, kind="ExternalOutput").ap()
        dbg["scr"] = nc.dram_tensor("dbg_scr", [2 * 8 * 3, S], BF16, kind="ExternalOutput").ap()
        dbg["qA"] = nc.dram_tensor("dbg_qA", [128, 512], BF16, kind="ExternalOutput").ap()
        dbg["kA"] = nc.dram_tensor("dbg_kA", [128, 512], BF16, kind="ExternalOutput").ap()
        dbg["VA"] = nc.dram_tensor("dbg_VA", [128, 512], BF16, kind="ExternalOutput").ap()

    w_in_v = w_in.rearrange("(dc p) n -> p dc n", p=128)
    w_kv_v = w_mem_kv.rearrange("(dc p) n -> p dc n", p=128)
    w_br_v = [w.rearrange("(ec p) n -> p ec n", p=128) for w in w_br]
    w_out_v = w_out.rearrange("(dc p) n -> p dc n", p=128)
    w_up_v = w_up.rearrange("(dc p) n -> p dc n", p=128)
    w_down_v = w_down.rearrange("(ft p) n -> p ft n", p=128)

    with contextlib.ExitStack() as es:
        P = Prog(nc, es)

        def sb(name, shape, dt):
            return es.enter_context(nc.sbuf_tensor(name, shape, dt))

        R1 = sb("R1", [128, 8 * S], BF16)
        R2 = sb("R2", [128, 12 * S], BF16)
        R3 = sb("R3", [128, 8 * S], BF16)
        R4 = sb("R4", [128, 8 * S], BF16)
        WT = [sb(f"WT{i}", [128, 4608], BF16) for i in range(2)]
        XT = [sb(f"XT{i}", [128, D], F32) for i in range(2)]
        GB = sb("GB", [128, D], F32)
        HB = [sb(f"HB{i}", [128, D], BF16) for i in range(2)]
        ET = [sb(f"ET{i}", [128, 512], F32) for i in range(2)]
        LP = [sb(f"LP{i}", [128, 512], BF16) for i in range(2)]
        WW = [sb(f"WW{i}", [128, 512], BF16) for i in range(2)]
        LS = [sb(f"LS{i}", [128, 512], BF16) for i in range(2)]
        RT = [sb(f"RT{i}", [128, 512], F32) for i in range(2)]
        SG = [ET[0], ET[1], RT[0]]
        MT = RT[1]
        BO = sb("BO", [128, 128], BF16)
        CST = sb("CST", [128, 7 * 128], BF16)
        ZR = sb("ZR", [128, 64], BF16)
        SS = sb("SS", [128, 40], F32)
        RS = sb("RS", [128, 40], F32)
        GQ = sb("GQ", [128, 4], F32)
        NB = sb("NB", [8, 1], F32)
        FLAST = sb("FLAST", [8, 4], F32)
        mhT = R4[:, 8192:10240].rearrange("p (a b) -> p a b", a=8)
        mkT = R4[:, 10240:11264].rearrange("p (a b) -> p a b", a=4)
        mv = R4[:, 11264:12288].rearrange("p (a b) -> p a b", a=2)
        FSP, FCS, FR, FON = ET[0][0:8, :], ET[1][0:8, :], RT[0][0:8, :], RT[1][0:8, :]
        FAP = [LP[0][0:8, :], LP[1][0:8, :], WW[0][0:8, :]]
        FAN = [WW[1][0:8, :], LS[0][0:8, :], LS[1][0:8, :]]

        PS = [es.enter_context(nc.psum_tensor(f"PS{i}", [128, 512], F32)) for i in range(7)]
        PT = es.enter_context(nc.psum_tensor("PT", [128, 8, 128], BF16))

        hT = R1[:, :].rearrange("p (a b) -> p a b", a=8)
        oT = R2[:, :].rearrange("p (a b) -> p a b", a=12)
        ident = CST[:, 0:128]
        uneg = CST[:, 128:256]
        negones = CST[:, 256:384]
        mask_sb = CST[:, 384:512]
        mask_fx = CST[:, 512:640]
        ones = CST[:, 640:768]
        blockones = BO[:, :]

        def hkeys(tc):
            return [("hT", t) for t in range(4 * tc, 4 * tc + 4)]

        ALLH = [("hT", t) for t in range(TT)]

        def pkey(ps):
            return ("PS", id(ps))

        def mm(out_, lhsT, rhs, start, stop, reads, writes):
            P.op("pe", lambda e: e.matmul(out_, lhsT, rhs, start=start, stop=stop), reads, writes)

        def wt_view(i, ncols):
            return WT[i][:, 0:8 * ncols].rearrange("p (a b) -> p a b", a=8)

        P.dma("pool", CST[:, :], consts[:, :], writes=["CST"])
        P.op("dve", lambda e: e.memset(ZR[:, :], 0.0), writes=["ZR"])
        P.op("dve", lambda e: e.memset(BO[:, :], 0.0), writes=["BO"])
        P.op("dve", lambda e: e.memset(BO[0:64, 0:64], 1.0), writes=["BO"])
        P.op("dve", lambda e: e.memset(BO[64:128, 64:128], 1.0), writes=["BO"])
        P.dma("sp", GQ[0:64, 0:1], g_fox_q[:, :], writes=["GQ"])
        P.dma("sp", GQ[64:128, 0:1], g_fox_q[:, :], writes=["GQ"])
        P.dma("sp", GQ[0:64, 1:2], g_fox_k[:, :], writes=["GQ"])
        P.dma("sp", GQ[64:128, 1:2], g_fox_k[:, :], writes=["GQ"])
        P.dma("sp", GQ[:, 2:3], g_mem_q[:, :], writes=["GQ"])
        P.dma("sp", GQ[:, 3:4], g_mem_k[:, :], writes=["GQ"])
        P.dma("sp", NB[:, :], b_forget[:, :], writes=["NB"])
        P.op("dve", lambda e: e.tensor_scalar_mul(out=GQ[:, 0:1], in0=GQ[:, 0:1], scalar1=0.125), reads=["GQ"], writes=["GQ"])
        P.op("dve", lambda e: e.tensor_scalar_mul(out=NB[:, :], in0=NB[:, :], scalar1=-1.0), reads=["NB"], writes=["NB"])

        def norm_transpose(src_rows, xt_i, col, dst, dst_keys, src_reads=(), x_loaded=False):
            xt = XT[xt_i]
            hb = HB[xt_i]
            if not x_loaded:
                P.dma("sp", xt[:, :], src_rows, reads=src_reads, writes=[("XT", xt_i)])
            P.op("act", lambda e: e.activation(out=hb[:, :], in_=xt[:, :], func=AF.Square, accum_out=SS[:, col:col + 1]),
                 reads=[("XT", xt_i)], writes=[("HB", xt_i), ("SS", col)])
            P.op("act", lambda e: e.activation(out=RS[:, col:col + 1], in_=SS[:, col:col + 1], func=AF.Ln, scale=1.0 / D, bias=EPS),
                 reads=[("SS", col)], writes=[("RS", col)])
            P.op("act", lambda e: e.activation(out=RS[:, col:col + 1], in_=RS[:, col:col + 1], func=AF.Exp, scale=-0.5),
                 reads=[("RS", col)], writes=[("RS", col)])
            P.op("dve", lambda e: e.scalar_tensor_tensor(out=hb[:, :], in0=xt[:, :], scalar=RS[:, col:col + 1], in1=GB[:, :],
                                                         op0=ALU.mult, op1=ALU.mult),
                 reads=[("XT", xt_i), ("RS", col), "GB"], writes=[("HB", xt_i)])
            for dc in range(8):
                P.op("pe", lambda e, dc=dc: e.transpose(PT[:, dc, :], hb[:, dc * 128:(dc + 1) * 128], ident),
                     reads=[("HB", xt_i), "CST"], writes=["PT"])
            P.op("act", lambda e: e.activation(out=dst, in_=PT[:, :, :], func=AF.Copy), reads=["PT"], writes=dst_keys)

        P.dma("sp", GB[:, :], g_mix[0:1, :].partition_broadcast(128), writes=["GB"])
        for tt in range(TT):
            norm_transpose(x[tt * 128:(tt + 1) * 128, :], tt % 2, tt, hT[:, :, tt * 128:(tt + 1) * 128], [("hT", tt)])
        if debug:
            P.dma("sp", dbg["hT"][:, :], R1[:, :], reads=ALLH)

        WF = WT[1][:, 4096:4160].rearrange("p (a b) -> p a b", a=8)
        P.op("dve", lambda e: e.memset(FON, 1.0), writes=["FON"])
        P.dma("pool", WF, w_in_v[:, :, C_F:C_F + 8], writes=[("WF",)])
        for tc in range(TC):
            ps = PS[tc % 2]
            for dc in range(8):
                mm(ps[0:8, :], WF[:, dc, :], hT[:, dc, tc * 512:(tc + 1) * 512], dc == 0, dc == 7, [("WF",)] + hkeys(tc), [pkey(ps)])
            P.op("act", lambda e, ps=ps: e.activation(out=FSP, in_=ps[0:8, :], func=AF.Exp, scale=-1.0, bias=NB[:, 0:1]),
                 reads=[pkey(ps), "NB"], writes=["FSP"])
            P.op("act", lambda e: e.activation(out=FSP, in_=FSP, func=AF.Ln, bias=1.0), reads=["FSP"], writes=["FSP"])
            if tc == 0:
                P.op("dve", lambda e: e.tensor_tensor_scan(out=FCS, data0=FON, data1=FSP, initial=0.0,
                                                           op0=ALU.mult, op1=ALU.add), reads=["FON", "FSP"], writes=["FCS"])
            else:
                P.op("dve", lambda e, tc=tc: e.tensor_tensor_scan(out=FCS, data0=FON, data1=FSP,
                                                                  initial=FLAST[:, tc - 1:tc], op0=ALU.mult, op1=ALU.add),
                     reads=["FON", "FSP", ("FLAST", tc - 1)], writes=["FCS"])
            P.op("dve", lambda e, tc=tc: e.tensor_copy(out=FLAST[:, tc:tc + 1], in_=FCS[:, 511:512]), reads=["FCS"], writes=[("FLAST", tc)])
            P.op("dve", lambda e: e.tensor_copy(out=FAP[0], in_=FCS), reads=["FCS"], writes=["FAP0"])
            P.op("dve", lambda e: e.tensor_tensor(out=FR, in0=FCS, in1=FAP[0], op=ALU.subtract), reads=["FCS", "FAP0"], writes=["FR"])
            P.op("dve", lambda e: e.tensor_copy(out=FAP[1], in_=FR), reads=["FR"], writes=["FAP1"])
            P.op("dve", lambda e: e.tensor_tensor(out=FR, in0=FR, in1=FAP[1], op=ALU.subtract), reads=["FR", "FAP1"], writes=["FR"])
            P.op("dve", lambda e: e.tensor_copy(out=FAP[2], in_=FR), reads=["FR"], writes=["FAP2"])
            for k in range(3):
                P.op("dve", lambda e, k=k: e.tensor_scalar_mul(out=FAN[k], in0=FAP[k], scalar1=-1.0), reads=[f"FAP{k}"], writes=[f"FAN{k}"])
                P.dma("sp", scr[0, :, k, tc * 512:(tc + 1) * 512], FAP[k], reads=[f"FAP{k}"], writes=["scr"])
                P.dma("sp", scr[1, :, k, tc * 512:(tc + 1) * 512], FAN[k], reads=[f"FAN{k}"], writes=["scr"])

        P.barrier()

        def proj_fm(ps, wv, fcol, tc, wkey):
            for dc in range(8):
                mm(ps[:, :], wv[:, dc, fcol * 128:(fcol + 1) * 128], hT[:, dc, tc * 512:(tc + 1) * 512],
                   dc == 0, dc == 7, [wkey] + hkeys(tc), [("PS", id(ps))])

        qT = R3[:, 0:4 * S].rearrange("p (a b) -> p a b", a=4)
        kT = R3[:, 4 * S:8 * S].rearrange("p (a b) -> p a b", a=4)
        vS = R4[:, 0:16 * 512].rearrange("p (a b) -> p a b", a=16)
        for which, c0, dst, scale in ((0, C_SBQ, qT, 0.125), (1, C_SBK, kT, 1.0)):
            P.dma("pool", wt_view(which, 512), w_in_v[:, :, c0:c0 + 512], writes=[("WT", which)])
            for ft in range(4):
                for tc in range(TC):
                    ps = PS[(ft * 4 + tc) % 2]
                    proj_fm(ps, wt_view(which, 512), ft, tc, ("WT", which))
                    P.op("act", lambda e, ps=ps, dst=dst, ft=ft, tc=tc, scale=scale:
                         e.activation(out=dst[:, ft, tc * 512:(tc + 1) * 512], in_=ps[:, :], func=AF.Copy, scale=scale),
                         reads=[pkey(ps)], writes=[("qk", which, ft, tc)])
        P.dma("pool", wt_view(0, 512), w_in_v[:, :, C_SBV:C_SBV + 512], writes=[("WT", 0)])
        for tt in range(TT):
            ps = PS[tt % 2]
            for dc in range(8):
                mm(ps[:, :], hT[:, dc, tt * 128:(tt + 1) * 128], wt_view(0, 512)[:, dc, :], dc == 0, dc == 7,
                   [("WT", 0), ("hT", tt)], [pkey(ps)])
            P.op("dve", lambda e, ps=ps, tt=tt: e.tensor_copy(out=vS[:, tt, :], in_=ps[:, :]), reads=[pkey(ps)], writes=[("v", tt)])

        def sb_stream(h, s):
            ft, pb = h // 2, 64 * (h % 2)
            A, O = PS[2 + s], PS[4 + s]
            E_, Lp, W_, Ls = ET[s], LP[s], WW[s], LS[s]
            kA, kO, kE, kL, kW, kS = pkey(A), pkey(O), ("ET", s), ("LP", s), ("WW", s), ("LS", s)
            for c in range(TC):
                qk_r = [("qk", 0, ft, c)]
                mm(O[0:64, :], ZR[:, 0:64], hT[:, 0, 0:512], True, False, ["ZR"] + hkeys(0), [kO])
                P.op("pool", lambda e: e.memset(Ls[:, :], 0.0), writes=[kS])
                yield
                top = 4 * c + 3
                for kb in range(top, -1, -1):
                    j = kb - 4 * c
                    q0 = 128 * j if j > 0 else 0
                    kq_r = [("qk", 1, ft, kb // 4)]
                    mm(A[:, q0:512], kT[pb:pb + 64, ft, kb * 128:(kb + 1) * 128],
                       qT[pb:pb + 64, ft, c * 512 + q0:(c + 1) * 512], True, j < 0, qk_r + kq_r, [kA])
                    if j >= 0:
                        mm(A[:, q0:q0 + 128], ident, mask_sb, False, True, ["CST"], [kA])
                    yield
                    P.op("act", lambda e, q0=q0: e.activation(out=E_[:, q0:512], in_=A[:, q0:512], func=AF.Exp),
                         reads=[kA], writes=[kE])
                    yield
                    P.op("act", lambda e, q0=q0: e.activation(out=Lp[:, q0:512], in_=E_[:, q0:512], func=AF.Ln, bias=1.0),
                         reads=[kE], writes=[kL])
                    yield
                    mm(A[:, q0:512], uneg, Lp[:, q0:512], False, kb == top, ["CST", kL], [kA])
                    if kb != top:
                        mm(A[:, q0:512], negones, Ls[:, q0:512], False, True, ["CST", kS], [kA])
                    yield
                    P.op("act", lambda e, q0=q0: e.activation(out=W_[:, q0:512], in_=A[:, q0:512], func=AF.Exp),
                         reads=[kA], writes=[kW])
                    if kb != 0:
                        P.op("dve", lambda e, q0=q0: e.tensor_tensor(out=Ls[:, q0:512], in0=Ls[:, q0:512], in1=Lp[:, q0:512], op=ALU.add),
                             reads=[kS, kL], writes=[kS])
                    yield
                    mm(O[0:64, q0:512], vS[:, kb, h * 64:(h + 1) * 64], W_[:, q0:512], False, kb == 0, [("v", kb), kW], [kO])
                    yield
                P.op("dve", lambda e, c=c: e.tensor_copy(out=oT[pb:pb + 64, ft, c * 512:(c + 1) * 512], in_=O[0:64, :]),
                     reads=[kO], writes=[("oT", ft, c, pb)])
                yield

        for hp in range(4):
            interleave([sb_stream(2 * hp, 0), sb_stream(2 * hp + 1, 1)])

        P.barrier()
        qA = R3[:, 0:4 * S].rearrange("p (a b) -> p a b", a=4)
        kA_ = R3[:, 4 * S:8 * S].rearrange("p (a b) -> p a b", a=4)
        VA = R4[:, 0:16 * 512].rearrange("p (a b c) -> p a b c", a=16, b=4)
        for half in range(2):
            for t_, name in ((qA, "qA"), (kA_, "kA")):
                P.op("pool", lambda e, t_=t_: e.memset(t_[64:128, :, :], 0.0), reads=[], writes=[(name, "aug")] + [(name, hl, tc) for hl in range(4) for tc in range(TC)])
                P.op("pool", lambda e, t_=t_: e.memset(t_[64:70, :, :], 1.0), reads=[], writes=[(name, "aug")])
            for hl in range(4):
                h = half * 4 + hl
                P.dma("sp", qA[64:67, hl, :], scr[1, h, :, :], reads=["scr"], writes=[("qA", "aug")])
                P.dma("sp", kA_[67:70, hl, :], scr[0, h, :, :], reads=["scr"], writes=[("kA", "aug")])
            if half == 0:
                P.op("pool", lambda e: e.memset(VA[:, :, :, 64:128], 1.0), reads=[], writes=[("VA", "ones")])
            for which, c0, dst, dname in ((0, C_FXQ, qA, "qA"), (1, C_FXK, kA_, "kA")):
                P.dma("pool", wt_view(which, 256), w_in_v[:, :, c0 + half * 256:c0 + half * 256 + 256], writes=[("WT", which)])
                for fl in range(2):
                    for tc in range(TC):
                        ps, ps2 = PS[0], PS[1]
                        proj_fm(ps, wt_view(which, 256), fl, tc, ("WT", which))
                        P.op("act", lambda e, ps=ps: e.activation(out=LP[0][:, :], in_=ps[:, :], func=AF.Square), reads=[pkey(ps)], writes=[("LP", 0)])
                        mm(ps2[:, :], blockones, LP[0][:, :], True, True, ["BO", ("LP", 0)], [pkey(ps2)])
                        for odd in range(2):
                            hl = 2 * fl + odd
                            pb = 64 * odd
                            rt = RT[odd]
                            P.op("act", lambda e, rt=rt, pb=pb, ps2=ps2: e.activation(out=rt[0:64, :], in_=ps2[pb:pb + 64, :], func=AF.Ln, scale=1.0 / 64, bias=EPS),
                                 reads=[pkey(ps2)], writes=[("RT", odd)])
                            P.op("act", lambda e, rt=rt: e.activation(out=rt[0:64, :], in_=rt[0:64, :], func=AF.Exp, scale=-0.5),
                                 reads=[("RT", odd)], writes=[("RT", odd)])
                            if odd:
                                P.op("act", lambda e, ps=ps: e.activation(out=ET[0][0:64, :], in_=ps[64:128, :], func=AF.Copy),
                                     reads=[pkey(ps)], writes=[("ET", 0)])
                                src, skey = ET[0][0:64, :], ("ET", 0)
                            else:
                                src, skey = ps[0:64, :], pkey(ps)
                            P.op("dve", lambda e, hl=hl, tc=tc, dst=dst, which=which, rt=rt, src=src:
                                 e.scalar_tensor_tensor(out=dst[0:64, hl, tc * 512:(tc + 1) * 512], in0=src,
                                                        scalar=GQ[0:64, which:which + 1], in1=rt[0:64, :],
                                                        op0=ALU.mult, op1=ALU.mult),
                                 reads=[skey, "GQ", ("RT", odd)], writes=[(dname, hl, tc)])
            P.dma("pool", wt_view(0, 256), w_in_v[:, :, C_FXV + half * 256:C_FXV + half * 256 + 256], writes=[("WT", 0)])
            for tt in range(TT):
                ps = PS[tt % 2]
                for dc in range(8):
                    mm(ps[:, 0:256], hT[:, dc, tt * 128:(tt + 1) * 128], wt_view(0, 256)[:, dc, :], dc == 0, dc == 7,
                       [("WT", 0), ("hT", tt)], [pkey(ps)])
                P.op("dve", lambda e, ps=ps, tt=tt: e.tensor_copy(out=VA[:, tt, :, 0:64], in_=ps[:, 0:256].rearrange("p (a b) -> p a b", a=4)),
                     reads=[pkey(ps)], writes=[("VA", tt)])

            def fox_stream(hl, s, half=half):
                h = half * 4 + hl
                ft, pb = h // 2, 64 * (h % 2)
                A, O = PS[2 + s], PS[4 + s]
                W_ = WW[s]
                kA, kO, kW, kR = pkey(A), pkey(O), ("WW", s), ("RT", s)
                for c in range(TC):
                    top = 4 * c + 3
                    for kb in range(0, top + 1):
                        j = kb - 4 * c
                        q0 = 128 * j if j > 0 else 0
                        mm(A[:, q0:512], kA_[:, hl, kb * 128:(kb + 1) * 128], qA[:, hl, c * 512 + q0:(c + 1) * 512], True, j < 0,
                           [("kA", hl, kb // 4), ("kA", "aug"), ("qA", hl, c), ("qA", "aug")], [kA])
                        if j >= 0:
                            mm(A[:, q0:q0 + 128], ident, mask_fx, False, True, ["CST"], [kA])
                        yield
                        P.op("act", lambda e, q0=q0: e.activation(out=W_[:, q0:512], in_=A[:, q0:512], func=AF.Exp), reads=[kA], writes=[kW])
                        yield
                        mm(O[:, q0:512], VA[:, kb, hl, :], W_[:, q0:512], kb == 0, kb == top, [("VA", kb), ("VA", "ones"), kW], [kO])
                        yield
                    P.op("act", lambda e: e.activation(out=RT[s][pb:pb + 64, :], in_=O[64:128, :], func=AF.Copy), reads=[kO], writes=[kR])
                    yield
                    P.op("dve", lambda e: e.reciprocal(out=RT[s][pb:pb + 64, :], in_=RT[s][pb:pb + 64, :]), reads=[kR], writes=[kR])
                    yield
                    P.op("dve", lambda e, c=c: e.tensor_tensor(out=oT[pb:pb + 64, 4 + ft, c * 512:(c + 1) * 512], in0=O[0:64, :],
                                                               in1=RT[s][pb:pb + 64, :], op=ALU.mult),
                         reads=[kO, kR], writes=[("oT", 4 + ft, c, pb)])
                    yield

            for hp in range(2):
                interleave([fox_stream(2 * hp, 0), fox_stream(2 * hp + 1, 1)])

        if debug:
            P.barrier()
            P.dma("sp", dbg["scr"][:, :], scr.rearrange("a b c s -> (a b c) s"), reads=[])
            P.dma("sp", dbg["qA"][:, :], qA[:, 0, 0:512], reads=[])
            P.dma("sp", dbg["kA"][:, :], kA_[:, 0, 0:512], reads=[])
            P.dma("sp", dbg["VA"][:, :], R4[:, 0:512], reads=[])
        P.barrier()
        P.dma("sp", GB[:, :], g_memn[0:1, :].partition_broadcast(128), reads=[], writes=["GB"])
        for mt in range(2):
            norm_transpose(mem[mt * 128:(mt + 1) * 128, :], mt, 16 + mt, mhT[:, :, mt * 128:(mt + 1) * 128], [("mhT", mt)])
        MH = [("mhT", 0), ("mhT", 1)]

        def head_norm128(ps, ps2, gcol, dst, dkeys, n):
            P.op("act", lambda e: e.activation(out=LP[0][:, 0:n], in_=ps[:, 0:n], func=AF.Square), reads=[pkey(ps)], writes=[("LP", 0)])
            mm(ps2[:, 0:n], ones, LP[0][:, 0:n], True, True, ["CST", ("LP", 0)], [pkey(ps2)])
            P.op("act", lambda e: e.activation(out=RT[0][:, 0:n], in_=ps2[:, 0:n], func=AF.Ln, scale=1.0 / 128, bias=EPS),
                 reads=[pkey(ps2)], writes=[("RT", 0)])
            P.op("act", lambda e: e.activation(out=RT[0][:, 0:n], in_=RT[0][:, 0:n], func=AF.Exp, scale=-0.5), reads=[("RT", 0)], writes=[("RT", 0)])
            P.op("dve", lambda e: e.scalar_tensor_tensor(out=dst, in0=ps[:, 0:n], scalar=GQ[:, gcol:gcol + 1], in1=RT[0][:, 0:n],
                                                         op0=ALU.mult, op1=ALU.mult),
                 reads=[pkey(ps), "GQ", ("RT", 0)], writes=dkeys)

        P.dma("pool", wt_view(0, 512), w_kv_v[:, :, 0:512], writes=[("WT", 0)])
        P.dma("pool", wt_view(1, 512), w_kv_v[:, :, 512:1024], writes=[("WT", 1)])
        for hd in range(4):
            ps, ps2 = PS[0], PS[1]
            for dc in range(8):
                mm(ps[:, 0:MEM], wt_view(0, 512)[:, dc, hd * 128:(hd + 1) * 128], mhT[:, dc, :], dc == 0, dc == 7, [("WT", 0)] + MH, [pkey(ps)])
            head_norm128(ps, ps2, 3, mkT[:, hd, :], [("mkT", hd)], MEM)
        for mt in range(2):
            ps = PS[mt]
            for dc in range(8):
                mm(ps[:, :], mhT[:, dc, mt * 128:(mt + 1) * 128], wt_view(1, 512)[:, dc, :], dc == 0, dc == 7, [("WT", 1)] + MH, [pkey(ps)])
            P.op("dve", lambda e, ps=ps, mt=mt: e.tensor_copy(out=mv[:, mt, :], in_=ps[:, :]), reads=[pkey(ps)], writes=[("mv", mt)])
        mqT = R3[:, 0:4 * S].rearrange("p (a b) -> p a b", a=4)
        P.dma("pool", wt_view(0, 512), w_in_v[:, :, C_MQ:C_MQ + 512], writes=[("WT", 0)])
        for hd in range(4):
            for tc in range(TC):
                ps, ps2 = PS[0], PS[1]
                proj_fm(ps, wt_view(0, 512), hd, tc, ("WT", 0))
                head_norm128(ps, ps2, 2, mqT[:, hd, tc * 512:(tc + 1) * 512], [("mqT", hd, tc)], 512)

        MSCALE = 128.0 ** -0.5

        def mem_stream(hd, s):
            A, O, Dn = PS[s], PS[2 + s], PS[4 + s]
            W_ = WW[s]
            kA, kO, kD, kW, kR = pkey(A), pkey(O), pkey(Dn), ("WW", s), ("RT", s)
            for c in range(TC):
                for mt in range(2):
                    mm(A[:, :], mkT[:, hd, mt * 128:(mt + 1) * 128], mqT[:, hd, c * 512:(c + 1) * 512], True, True,
                       [("mkT", hd), ("mqT", hd, c)], [kA])
                    yield
                    P.op("act", lambda e: e.activation(out=W_[:, :], in_=A[:, :], func=AF.Exp, scale=MSCALE), reads=[kA], writes=[kW])
                    yield
                    mm(O[:, :], mv[:, mt, hd * 128:(hd + 1) * 128], W_[:, :], mt == 0, mt == 1, [("mv", mt), kW], [kO])
                    mm(Dn[:, :], ones, W_[:, :], mt == 0, mt == 1, ["CST", kW], [kD])
                    yield
                P.op("dve", lambda e: e.reciprocal(out=RT[s][:, :], in_=Dn[:, :]), reads=[kD], writes=[kR])
                yield
                P.op("dve", lambda e, c=c: e.tensor_tensor(out=oT[:, 8 + hd, c * 512:(c + 1) * 512], in0=O[:, :], in1=RT[s][:, :], op=ALU.mult),
                     reads=[kO, kR], writes=[("oT", 8 + hd, c, 0)])
                yield

        for hp in range(2):
            interleave([mem_stream(2 * hp, 0), mem_stream(2 * hp + 1, 1)])

        if debug:
            P.barrier()
            P.dma("sp", dbg["oT"][:, :], R2[:, :], reads=[])

        P.barrier()
        mT = R3[:, :].rearrange("p (a b) -> p a b", a=8)
        for j in range(8):
            wi = j % 2
            wg = WT[wi][:, 0:3072].rearrange("p (b a c) -> p b a c", b=3, a=8)
            wb = WT[wi][:, 3072:4608].rearrange("p (b a c) -> p b a c", b=3, a=4)
            for b in range(3):
                P.dma("pool", wg[:, b, :, :], w_in_v[:, :, C_G + b * 1024 + j * 128:C_G + b * 1024 + (j + 1) * 128], writes=[("WT", wi)])
                P.dma("pool", wb[:, b, :, :], w_br_v[b][:, :, j * 128:(j + 1) * 128], writes=[("WT", wi)])
            for c in range(TC):
                for b in range(3):
                    G, BR = PS[b], PS[3 + b]
                    for dc in range(8):
                        mm(G[:, :], wg[:, b, dc, :], hT[:, dc, c * 512:(c + 1) * 512], dc == 0, dc == 7, [("WT", wi)] + hkeys(c), [pkey(G)])
                    for ec in range(4):
                        mm(BR[:, :], wb[:, b, ec, :], oT[:, 4 * b + ec, c * 512:(c + 1) * 512], ec == 0, ec == 3, [("WT", wi)], [pkey(BR)])
                    P.op("act", lambda e, b=b, G=G: e.activation(out=SG[b][:, :], in_=G[:, :], func=AF.Sigmoid), reads=[pkey(G)], writes=[("SG", b)])
                    if b == 0:
                        P.op("dve", lambda e, BR=BR: e.tensor_tensor(out=MT[:, :], in0=BR[:, :], in1=SG[0][:, :], op=ALU.mult),
                             reads=[pkey(BR), ("SG", 0)], writes=["MT"])
                    else:
                        P.op("dve", lambda e, b=b, BR=BR: e.tensor_tensor(out=SG[b][:, :], in0=BR[:, :], in1=SG[b][:, :], op=ALU.mult),
                             reads=[pkey(BR), ("SG", b)], writes=[("SG", b)])
                        if b == 1:
                            P.op("pool", lambda e: e.tensor_tensor(out=MT[:, :], in0=MT[:, :], in1=SG[1][:, :], op=ALU.add),
                                 reads=["MT", ("SG", 1)], writes=["MT"])
                        else:
                            P.op("pool", lambda e, j=j, c=c: e.tensor_tensor(out=mT[:, j, c * 512:(c + 1) * 512], in0=MT[:, :], in1=SG[2][:, :], op=ALU.add),
                                 reads=["MT", ("SG", 2)], writes=[("mT", j, c)])
        if debug:
            P.barrier()
            P.dma("sp", dbg["mT"][:, :], R3[:, :], reads=[])

        P.barrier()
        WO = R4[:, 0:8 * D].rearrange("p (a b) -> p a b", a=8)
        WD_A = R2[:, 0:24 * D].rearrange("p (a b) -> p a b", a=24)
        WD_B = R4[:, 8 * D:16 * D].rearrange("p (a b) -> p a b", a=8)
        h2T = hT

        def wdown(ft):
            return WD_A[:, ft, :] if ft < 24 else WD_B[:, ft - 24, :]

        for hf in range(2):
            P.dma("pool", WO[:, :, hf * 512:(hf + 1) * 512], w_out_v[:, :, hf * 512:(hf + 1) * 512], writes=[("WO", hf)])
        P.dma("sp", GB[:, :], g_mlp[0:1, :].partition_broadcast(128), reads=[], writes=["GB"])
        for tt in range(TT):
            xi = tt % 2
            P.dma("sp", XT[xi][:, :], x[tt * 128:(tt + 1) * 128, :], writes=[("XT", xi)])
            for hf in range(2):
                ps = PS[hf]
                for dc in range(8):
                    mm(ps[:, :], mT[:, dc, tt * 128:(tt + 1) * 128], WO[:, dc, hf * 512:(hf + 1) * 512], dc == 0, dc == 7,
                       [("WO", hf), ("mT", dc, tt // 4)], [pkey(ps)])
                P.op("dve", lambda e, ps=ps, xi=xi, hf=hf: e.tensor_tensor(out=XT[xi][:, hf * 512:(hf + 1) * 512], in0=ps[:, :],
                                                                             in1=XT[xi][:, hf * 512:(hf + 1) * 512], op=ALU.add),
                     reads=[pkey(ps), ("XT", xi)], writes=[("XT", xi)])
            P.dma("sp", out[tt * 128:(tt + 1) * 128, :], XT[xi][:, :], reads=[("XT", xi)], writes=[("out", tt)])
            norm_transpose(None, xi, 20 + tt, h2T[:, :, tt * 128:(tt + 1) * 128], [("hT", tt)], x_loaded=True)
            if tt == 3:
                for g in range(8):
                    dstv = WD_A[:, 4 * g:4 * g + 4, :] if g < 6 else WD_B[:, 4 * (g - 6):4 * (g - 6) + 4, :]
                    P.dma("pool", dstv, w_down_v[:, 4 * g:4 * g + 4, :], writes=[("WD", g)])

        P.barrier()
        u2 = R3[:, 0:32 * 512].rearrange("p (a b) -> p a b", a=32)
        WG = [wt_view(0, 512), wt_view(1, 512),
              R4[:, 0:4096].rearrange("p (a b) -> p a b", a=8), R4[:, 4096:8192].rearrange("p (a b) -> p a b", a=8)]
        nload = 0
        for c in range(TC):
            for fq in range(8):
                wi = nload % 4
                nload += 1
                wv = WG[wi]
                P.dma("pool", wv, w_up_v[:, :, fq * 512:(fq + 1) * 512], writes=[("WG", wi)])
                for fl in range(4):
                    ft = fq * 4 + fl
                    ps = PS[ft % 2]
                    rl = LP[ft % 2]
                    for dc in range(8):
                        mm(ps[:, :], wv[:, dc, fl * 128:(fl + 1) * 128], h2T[:, dc, c * 512:(c + 1) * 512], dc == 0, dc == 7,
                           [("WG", wi)] + hkeys(c), [pkey(ps)])
                    P.op("act", lambda e, ps=ps, rl=rl: e.activation(out=rl[:, :], in_=ps[:, :], func=AF.Relu), reads=[pkey(ps)], writes=[("LP", ft % 2)])
                    P.op("pool", lambda e, rl=rl, ft=ft: e.tensor_tensor(out=u2[:, ft, :], in0=rl[:, :], in1=rl[:, :], op=ALU.mult),
                         reads=[("LP", ft % 2)], writes=[("u2", ft)])
            for hf in range(2):
                for tl in range(4):
                    tt = 4 * c + tl
                    ps = PS[2 + tl]
                    xi = tl % 2
                    P.dma("sp", XT[xi][:, 0:512], out[tt * 128:(tt + 1) * 128, hf * 512:(hf + 1) * 512], reads=[("out", tt)], writes=[("XT", xi)])
                    for ft in range(32):
                        mm(ps[:, :], u2[:, ft, tl * 128:(tl + 1) * 128], wdown(ft)[:, hf * 512:(hf + 1) * 512], ft == 0, ft == 31,
                           [("u2", ft), ("WD", ft // 4)], [pkey(ps)])
                    P.op("dve", lambda e, ps=ps, xi=xi: e.tensor_tensor(out=XT[xi][:, 0:512], in0=ps[:, :], in1=XT[xi][:, 0:512], op=ALU.add),
                         reads=[pkey(ps), ("XT", xi)], writes=[("XT", xi)])
                    P.dma("sp", out[tt * 128:(tt + 1) * 128, hf * 512:(hf + 1) * 512], XT[xi][:, 0:512], reads=[("XT", xi)], writes=[("out", tt)])

        P.run()
    return nc


def make_consts():
    i = np.arange(128)
    ident = np.eye(128, dtype=np.float32)
    uneg = -(i[:, None] >= i[None, :]).astype(np.float32)
    negones = -np.ones((128, 128), np.float32)
    mask_sb = np.where(i[:, None] >= i[None, :], NEG, 0.0).astype(np.float32)
    mask_fx = np.where(i[:, None] > i[None, :], NEG, 0.0).astype(np.float32)
    ones = np.ones((128, 128), np.float32)
    blk = np.zeros((128, 128), np.float32)
    blk[:64, :64] = 1.0
    blk[64:, 64:] = 1.0
    return np.concatenate([ident, uneg, negones, mask_sb, mask_fx, ones, blk], axis=1)


_NC_CACHE = {}


def kernel(x, mem, g_mix_norm, g_mem_norm, w_in, b_forget, g_fox_q, g_fox_k, g_mem_q, g_mem_k,
           w_mem_kv, w_branch_sb, w_branch_fox, w_branch_mem, w_out, g_mlp_norm, w_ff_up, w_ff_down):
    debug = bool(os.environ.get("MK_DEBUG"))
    f = lambda a: np.ascontiguousarray(np.asarray(a, dtype=np.float32))
    x = f(x)
    mem = f(mem)
    shared = {
        "g_mix_norm": f(g_mix_norm)[0].reshape(1, D),
        "g_mem_norm": f(g_mem_norm)[0].reshape(1, D),
        "w_in": f(w_in)[0],
        "b_forget": f(b_forget)[0].reshape(8, 1),
        "g_fox_q": f(g_fox_q)[0].reshape(64, 1),
        "g_fox_k": f(g_fox_k)[0].reshape(64, 1),
        "g_mem_q": f(g_mem_q)[0].reshape(128, 1),
        "g_mem_k": f(g_mem_k)[0].reshape(128, 1),
        "w_mem_kv": f(w_mem_kv)[0],
        "w_branch_sb": f(w_branch_sb)[0],
        "w_branch_fox": f(w_branch_fox)[0],
        "w_branch_mem": f(w_branch_mem)[0],
        "w_out": f(w_out)[0],
        "g_mlp_norm": f(g_mlp_norm)[0].reshape(1, D),
        "w_ff_up": f(w_ff_up)[0],
        "w_ff_down": f(w_ff_down)[0],
        "consts": make_consts(),
    }
    nc = build_nc(debug)
    ncores = int(os.environ.get("MK_CORES", NCORES)) if debug else NCORES
    in_maps = [dict(shared, x=x[b], mem=mem[b]) for b in range(ncores)]
    res = run_bass_kernel_spmd(nc, in_maps, core_ids=list(range(ncores)))
    if debug:
        kernel.debug = res.results
    return np.stack([np.asarray(r["out"], dtype=np.float32) for r in res.results], axis=0)
```
